# Optimizing a Trainium2 kernel written in Bass

```python
import jax, jax.numpy as jnp
from jax import lax
import numpy as np

D_MODEL = 2048
BATCH = 16
SEQ = 2048
DEPTH = 4
DEC_BATCH = 1
DEC_SEQ = 8192
PAST_LEN = 128

A_WIDTH = D_MODEL // 2
A_HEAD_DIM = 64
N_A_HEADS = A_WIDTH // A_HEAD_DIM
B_WIDTH = D_MODEL - A_WIDTH
B_HEAD_DIM = 64
N_B_HEADS = B_WIDTH // B_HEAD_DIM
CHUNK = 128
AB_IN = A_WIDTH + 2 * B_WIDTH
CONV_WIDTH = 3
C_WIDTH = D_MODEL
D_FF = ((8 * D_MODEL // 3 + 255) // 256) * 256
PLE_DIM = 256
N_EVEN = (DEPTH + 1) // 2
N_ODD = DEPTH // 2
EPS = 1e-6

kernel_name = 'hybrid_fourier_sgu_shortconv_encoder'


def rmsnorm(x, g):
    xf = x.astype(jnp.float32)
    y = xf * lax.rsqrt(jnp.mean(xf * xf, axis=-1, keepdims=True) + EPS)
    return (y * g.astype(jnp.float32)).astype(x.dtype)


def swiglu(h, w_in, w_out):
    gate, up = jnp.split(h @ w_in, 2, axis=-1)
    return (jax.nn.silu(gate) * up) @ w_out


def fourier_sgu_mixer(h, w_in, g_v, w_s, b_s, w_out):
    bsz, s, _ = h.shape
    z = h @ w_in
    za = z[..., :A_WIDTH].reshape(bsz, s, N_A_HEADS, A_HEAD_DIM)
    ya = jnp.fft.fft2(za.astype(jnp.float32), axes=(1, 3), norm='ortho').real.astype(h.dtype)
    zb = jax.nn.gelu(z[..., A_WIDTH:])
    u = zb[..., :B_WIDTH].reshape(bsz, s, N_B_HEADS, B_HEAD_DIM)
    v = rmsnorm(zb[..., B_WIDTH:].reshape(bsz, s, N_B_HEADS, B_HEAD_DIM), g_v)
    vc = v.reshape(bsz, s // CHUNK, CHUNK, N_B_HEADS, B_HEAD_DIM)
    mixed = jnp.einsum('hts,bnshc->bnthc', w_s, vc) + b_s.T[None, None, :, :, None]
    yb = u * mixed.reshape(bsz, s, N_B_HEADS, B_HEAD_DIM)
    y = jnp.concatenate([ya.reshape(bsz, s, A_WIDTH), yb.reshape(bsz, s, B_WIDTH)], axis=-1)
    return y @ w_out


def short_conv_mixer(h, w_in, w_conv, b_conv, w_out):
    gb, gc, xin = jnp.split(h @ w_in, 3, axis=-1)
    z = gc * xin
    zp = jnp.pad(z, ((0, 0), (1, 1), (0, 0)))
    conv = zp[:, :-2] * w_conv[0] + zp[:, 1:-1] * w_conv[1] + zp[:, 2:] * w_conv[2] + b_conv
    return (gb * conv) @ w_out


def trunk(x, p, g_ffn1, w_ffn1_in, w_ffn1_out, g_mix, w_in_ab, g_v, w_s, b_s, w_out_ab,
          w_in_c, w_conv, b_conv, w_out_c, g_ffn2, w_ffn2_in, w_ffn2_out,
          g_ple, w_ple_gate, w_ple, g_final):
    for i in range(DEPTH):
        x = x + 0.5 * swiglu(rmsnorm(x, g_ffn1[i]), w_ffn1_in[i], w_ffn1_out[i])
        h = rmsnorm(x, g_mix[i])
        j = i // 2
        if i % 2 == 0:
            x = x + fourier_sgu_mixer(h, w_in_ab[j], g_v[j], w_s[j], b_s[j], w_out_ab[j])
        else:
            x = x + short_conv_mixer(h, w_in_c[j], w_conv[j], b_conv[j], w_out_c[j])
        x = x + 0.5 * swiglu(rmsnorm(x, g_ffn2[i]), w_ffn2_in[i], w_ffn2_out[i])
        gate = jax.nn.sigmoid(rmsnorm(x, g_ple[i]) @ w_ple_gate[i])
        x = x + gate * (p[i] @ w_ple[i])
    return rmsnorm(x, g_final)


def setup_inputs(seed: int = 0) -> dict:
    key = jax.random.key(seed)
    ks = jax.random.split(key, 24)

    def nrm(k, shape, scale):
        return jax.random.normal(k, shape, jnp.float32) * scale

    def gain(k, shape):
        return 1.0 + 0.02 * jax.random.normal(k, shape, jnp.float32)

    return {
        'x_prompt': nrm(ks[0], (BATCH, SEQ, D_MODEL), 1.0),
        'x_sample': nrm(ks[1], (DEC_BATCH, DEC_SEQ, D_MODEL), 1.0),
        'p_prompt': nrm(ks[2], (DEPTH, BATCH, SEQ, PLE_DIM), 1.0),
        'p_sample': nrm(ks[3], (DEPTH, DEC_BATCH, DEC_SEQ, PLE_DIM), 1.0),
        'g_ffn1': gain(ks[4], (DEPTH, D_MODEL)),
        'w_ffn1_in': nrm(ks[5], (DEPTH, D_MODEL, 2 * D_FF), D_MODEL ** -0.5),
        'w_ffn1_out': nrm(ks[6], (DEPTH, D_FF, D_MODEL), D_FF ** -0.5),
        'g_mix': gain(ks[7], (DEPTH, D_MODEL)),
        'w_in_ab': nrm(ks[8], (N_EVEN, D_MODEL, AB_IN), D_MODEL ** -0.5),
        'g_v': gain(ks[9], (N_EVEN, N_B_HEADS, B_HEAD_DIM)),
        'w_s': nrm(ks[10], (N_EVEN, N_B_HEADS, CHUNK, CHUNK), CHUNK ** -0.5),
        'b_s': 1.0 + 0.1 * jax.random.normal(ks[11], (N_EVEN, N_B_HEADS, CHUNK), jnp.float32),
        'w_out_ab': nrm(ks[12], (N_EVEN, A_WIDTH + B_WIDTH, D_MODEL), (A_WIDTH + B_WIDTH) ** -0.5),
        'w_in_c': nrm(ks[13], (N_ODD, D_MODEL, 3 * C_WIDTH), D_MODEL ** -0.5),
        'w_conv': nrm(ks[14], (N_ODD, CONV_WIDTH, C_WIDTH), CONV_WIDTH ** -0.5),
        'b_conv': nrm(ks[15], (N_ODD, C_WIDTH), 0.01),
        'w_out_c': nrm(ks[16], (N_ODD, C_WIDTH, D_MODEL), C_WIDTH ** -0.5),
        'g_ffn2': gain(ks[17], (DEPTH, D_MODEL)),
        'w_ffn2_in': nrm(ks[18], (DEPTH, D_MODEL, 2 * D_FF), D_MODEL ** -0.5),
        'w_ffn2_out': nrm(ks[19], (DEPTH, D_FF, D_MODEL), D_FF ** -0.5),
        'g_ple': gain(ks[20], (DEPTH, D_MODEL)),
        'w_ple_gate': nrm(ks[21], (DEPTH, D_MODEL, D_MODEL), D_MODEL ** -0.5),
        'w_ple': nrm(ks[22], (DEPTH, PLE_DIM, D_MODEL), PLE_DIM ** -0.5),
        'g_final': gain(ks[23], (D_MODEL,)),
    }


def reference(x_prompt, x_sample, p_prompt, p_sample, g_ffn1, w_ffn1_in, w_ffn1_out, g_mix,
              w_in_ab, g_v, w_s, b_s, w_out_ab, w_in_c, w_conv, b_conv, w_out_c,
              g_ffn2, w_ffn2_in, w_ffn2_out, g_ple, w_ple_gate, w_ple, g_final):
    weights = (g_ffn1, w_ffn1_in, w_ffn1_out, g_mix, w_in_ab, g_v, w_s, b_s, w_out_ab,
               w_in_c, w_conv, b_conv, w_out_c, g_ffn2, w_ffn2_in, w_ffn2_out,
               g_ple, w_ple_gate, w_ple, g_final)
    y_prompt = trunk(x_prompt, p_prompt, *weights)
    y_sample = trunk(x_sample, p_sample, *weights)
    return (y_prompt, y_sample)
```

```python
import numpy as np
import ml_dtypes
import concourse.bass as bass
import concourse.mybir as mybir
from concourse.bass_utils import run_bass_kernel_spmd

F32 = mybir.dt.float32
BF16 = mybir.dt.bfloat16
AF = mybir.ActivationFunctionType
ALU = mybir.AluOpType
AX = mybir.AxisListType

NCORES = 8
D = 2048
KC = 16
DFF = 5632
FC = 44
T = 512
NTILE = 10
DEPTH = 4
EPS = 1e-6
ENGS = ("pe", "act", "dve", "pool", "sp")
NSLOT = 6
PADL = 16


class Buf:
    __slots__ = ("name", "last_w", "readers")

    def __init__(self, name):
        self.name = name
        self.last_w = None
        self.readers = {}


class Op:
    __slots__ = ("eng", "fn", "waits", "signal", "sig_no", "idx", "dma_sem", "dma_val", "dma_inc")

    def __init__(self, eng, fn, idx):
        self.eng = eng
        self.fn = fn
        self.idx = idx
        self.waits = []
        self.signal = False
        self.sig_no = None
        self.dma_sem = None
        self.dma_val = None
        self.dma_inc = 16


class Sched:
    def __init__(self, nc):
        self.nc = nc
        self.ops = {e: [] for e in ENGS}
        self.prog_sem = {e: nc.alloc_semaphore(name=f"prog_{e}") for e in ENGS}
        self.dma_cnt = {}
        self.seen_eng = {e: {f: -1 for f in ENGS} for e in ENGS}
        self.seen_dma = {e: {} for e in ENGS}
        self.n_sems = 0

    def new_dma_sem(self, name=None):
        self.n_sems += 1
        s = self.nc.alloc_semaphore(name=name or f"dsem{self.n_sems}")
        self.dma_cnt[s] = 0
        return s

    def _need(self, op, tok, same_sync):
        e = op.eng
        if tok[0] == "op":
            p = tok[1]
            if p.eng == e and not same_sync:
                return
            if self.seen_eng[e][p.eng] >= p.idx:
                return
            self.seen_eng[e][p.eng] = p.idx
            p.signal = True
            op.waits.append(tok)
        else:
            _, sem, val = tok
            if self.seen_dma[e].get(sem, 0) >= val:
                return
            self.seen_dma[e][sem] = val
            op.waits.append(tok)

    def add(self, eng, fn, reads=(), writes=(), dma_sem=None, same_sync=None, dma_inc=16):
        if same_sync is None:
            same_sync = eng != "pe"
        op = Op(eng, fn, len(self.ops[eng]))
        for b in reads:
            if b.last_w is not None:
                self._need(op, b.last_w, same_sync)
        for b in writes:
            if b.last_w is not None:
                self._need(op, b.last_w, same_sync)
            for t in b.readers.values():
                self._need(op, t, same_sync)
        if dma_sem is not None:
            self.dma_cnt[dma_sem] += dma_inc
            op.dma_sem = dma_sem
            op.dma_val = self.dma_cnt[dma_sem]
            op.dma_inc = dma_inc
            tok = ("dma", dma_sem, op.dma_val)
        else:
            tok = ("op", op)
        for b in writes:
            b.last_w = tok
            b.readers = {}
        for b in reads:
            if b not in writes:
                key = eng if tok[0] == "op" else tok[1]
                b.readers[key] = tok
        self.ops[eng].append(op)
        return tok

    def finalize(self):
        for e in ENGS:
            n = 0
            for op in self.ops[e]:
                if op.signal:
                    n += 1
                    op.sig_no = n

    def _wait(self, h, tok):
        if tok[0] == "op":
            h.wait_ge(self.prog_sem[tok[1].eng], tok[1].sig_no)
        else:
            h.wait_ge(tok[1], tok[2])

    def replay(self, eng, h):
        for op in self.ops[eng]:
            for tok in op.waits:
                self._wait(h, tok)
            ins = op.fn(h)
            if op.dma_sem is not None:
                ins.then_inc(op.dma_sem, op.dma_inc)
            elif op.signal:
                ins.then_inc(self.prog_sem[eng], 1)

    def run(self, final_tokens=()):
        for tok in final_tokens:
            if tok[0] == "op":
                tok[1].signal = True
        self.finalize()
        nc = self.nc
        with nc.Block() as block:
            @block.tensor
            def _(h):
                self.replay("pe", h)

            @block.scalar
            def _(h):
                self.replay("act", h)

            @block.vector
            def _(h):
                self.replay("dve", h)

            @block.gpsimd
            def _(h):
                self.replay("pool", h)

            @block.sync
            def _(h):
                self.replay("sp", h)
                for tok in final_tokens:
                    self._wait(h, tok)


def build_program(segs=None):
    nc = bass.Bass("TRN2", target_bir_lowering=False)

    def din(name, shape, dt=F32):
        return nc.dram_tensor(name, list(shape), dt, kind="ExternalInput").ap()

    xin = din("xin", [NTILE * T, D])
    pin = din("pin", [DEPTH, NTILE * T, 256])
    g_ffn1 = din("g_ffn1", [4, D]); w_ffn1_in = din("w_ffn1_in", [4, D, 2 * DFF]); w_ffn1_out = din("w_ffn1_out", [4, DFF, D])
    g_mix = din("g_mix", [4, D]); w_in_ab = din("w_in_ab", [2, D, 3072]); g_v = din("g_v", [2, 16, 64])
    w_s = din("w_s", [2, 16, 128, 128]); b_s = din("b_s", [2, 16, 128]); w_out_ab = din("w_out_ab", [2, D, D])
    w_in_c = din("w_in_c", [2, D, 3 * D]); w_conv = din("w_conv", [2, 3, D]); b_conv = din("b_conv", [2, D])
    w_out_c = din("w_out_c", [2, D, D])
    g_ffn2 = din("g_ffn2", [4, D]); w_ffn2_in = din("w_ffn2_in", [4, D, 2 * DFF]); w_ffn2_out = din("w_ffn2_out", [4, DFF, D])
    g_ple = din("g_ple", [4, D]); w_ple_gate = din("w_ple_gate", [4, D, D]); w_ple = din("w_ple", [4, 256, D])
    g_final = din("g_final", [D])
    dftp_c = din("dftp_c", [2048, 2048], BF16); dftp_s = din("dftp_s", [2048, 2048], BF16)
    dfts_c = din("dfts_c", [8192, 1024], BF16); dfts_s = din("dfts_s", [8192, 1024], BF16)
    d64_in = din("d64", [128, 256], BF16)
    ident_in = din("identf", [128, 128])
    selh_in = din("selh", [16, 2])
    yout = nc.dram_tensor("yout", [NTILE * T, D], F32, kind="ExternalOutput").ap()

    xs_dram = nc.dram_tensor("xs_dram", [NTILE, 128, KC, T], F32).ap()
    xcs_p = [nc.dram_tensor(f"xcs_p{i}", [2048, 2048], BF16).ap() for i in range(2)]
    xcs_sl = nc.dram_tensor("xcs_sl", [1024, 2048], BF16)
    xcs_sg = nc.dram_tensor("xcs_sg", [8192, 2048], BF16)
    zs_p = [nc.dram_tensor(f"zs_p{i}", [128, KC, 2048 + 2 * PADL], F32).ap() for i in range(2)]
    zs_s = nc.dram_tensor("zs_s", [128, KC, 1024 + 2 * PADL], F32).ap()
    gbs = nc.dram_tensor("gbs", [NTILE, 128, KC, T], F32).ap()
    edge_l = nc.dram_tensor("edge_l", [2, D], F32)
    edge_g = nc.dram_tensor("edge_g", [16, D], F32)

    S = Sched(nc)

    x_t = nc.alloc_sbuf_tensor("x_t", [128, KC, T], F32)
    h_t = nc.alloc_sbuf_tensor("h_t", [128, KC, T], BF16)
    R = nc.alloc_sbuf_tensor("R", [128, 14336], F32)
    slots = [nc.alloc_sbuf_tensor(f"slot{i}", [128, 4096], BF16) for i in range(NSLOT)]
    rstd = nc.alloc_sbuf_tensor("rstd", [128, T], F32)
    sq = [nc.alloc_sbuf_tensor(f"sq{i}", [128, T], BF16) for i in range(4)]
    ones_b = nc.alloc_sbuf_tensor("ones_b", [128, 128], BF16)
    identf = nc.alloc_sbuf_tensor("identf_sb", [128, 128], F32)
    d64 = nc.alloc_sbuf_tensor("d64_sb", [128, 256], BF16)
    gvec = nc.alloc_sbuf_tensor("gvec", [128, 17, KC], F32)
    convw = nc.alloc_sbuf_tensor("convw", [128, 2, 4, KC], F32)
    wsT = nc.alloc_sbuf_tensor("wsT", [128, 16, 128], BF16)
    bsT = nc.alloc_sbuf_tensor("bsT", [128, 8, 128], F32)
    gva = nc.alloc_sbuf_tensor("gva", [128, 1024], F32)
    gvb = nc.alloc_sbuf_tensor("gvb", [128, 1024], F32)
    ss16 = nc.alloc_sbuf_tensor("ss16", [128, 16], F32)
    rs16 = nc.alloc_sbuf_tensor("rs16", [128, 16], F32)
    eg_sb = nc.alloc_sbuf_tensor("eg_sb", [16, D], F32)
    selh = nc.alloc_sbuf_tensor("selh_sb", [16, 2], F32)
    halo = nc.alloc_sbuf_tensor("halo", [128, KC, 2], F32)
    zero_t = nc.alloc_sbuf_tensor("zero_t", [128, KC, 1], F32)
    dummy = nc.alloc_sbuf_tensor("dummy_t", [128, 8], F32)
    eps_t = nc.alloc_sbuf_tensor("eps_t", [128, 1], F32)
    pw_t = nc.alloc_sbuf_tensor("pw_t", [128, 2, D], BF16)
    edge_sb = nc.alloc_sbuf_tensor("edge_sb", [128, 2, KC], F32)
    b_edge_sb = Buf("edge_sb")
    b_pw = Buf("pw")
    PS = nc.alloc_psum_tensor("PS", [128, 8, T], F32)

    def bank(b):
        return PS[:, b, :]

    pb = [Buf(f"pb{i}") for i in range(8)]
    xb = [Buf(f"xb{i}") for i in range(KC)]
    hb = [Buf(f"hb{i}") for i in range(KC)]
    slotb = [Buf(f"slot{i}") for i in range(NSLOT)]
    slot_sem = [S.new_dma_sem(f"slotsem{i}") for i in range(NSLOT)]
    b_rstd = Buf("rstd")
    sqb = [Buf(f"sq{i}") for i in range(4)]
    b_const = Buf("const")
    b_evc = Buf("evenconst")
    b_ss = Buf("ss16")
    b_rs = Buf("rs16")
    b_dummy = Buf("dummy")
    sem_misc = S.new_dma_sem("misc")
    sem_x = S.new_dma_sem("xld")
    sem_xst = S.new_dma_sem("xst")
    sem_out = S.new_dma_sem("out")
    sem_scr = [S.new_dma_sem(f"scr{i}") for i in range(8)]
    sem_stage = [S.new_dma_sem(f"stg{i}") for i in range(8)]
    sem_cc = S.new_dma_sem("cc")
    sem_pwl = S.new_dma_sem("pwl")
    sem_tok = [S.new_dma_sem(f"tok{i}") for i in range(8)]

    b_xs = [Buf(f"xs{i}") for i in range(NTILE)]
    b_xcs = [Buf("xcs_p0"), Buf("xcs_p1"), Buf("xcs_sl")]
    b_xcs_g = Buf("xcs_sg")
    b_zs = [Buf("zs0"), Buf("zs1"), Buf("zs_s")]
    b_gbs = [Buf(f"gbs{i}") for i in range(NTILE)]
    b_edge_l = Buf("edge_l")
    b_edge_g = Buf("edge_g")
    b_halo = Buf("halo")
    b_eg = Buf("eg_sb")

    def rv(off_b, nbytes, dt, pattern=None, **kw):
        a = R[:, off_b // 4:(off_b + nbytes) // 4]
        if dt == BF16:
            a = a.bitcast(BF16)
        if pattern:
            a = a.rearrange(pattern, **kw)
        return a

    hid = rv(0, 45056, BF16, "p (k t) -> p k t", t=T)
    hidb = [Buf(f"hid{j}") for j in range(FC)]
    sgt = [rv(45056 + i * 2048, 2048, F32) for i in range(2)]
    sgb = [Buf(f"sg{i}") for i in range(2)]
    tok = rv(0, 32768, F32, "p (c d) -> p c d", d=D)
    tokb = [Buf(f"tok{i}") for i in range(4)]
    uT = rv(0, 16384, F32, "p (k t) -> p k t", t=T); uTb = [Buf(f"uT{i}") for i in range(8)]
    zaT = rv(16384, 8192, BF16, "p (k t) -> p k t", t=T); zaTb = [Buf(f"zaT{i}") for i in range(8)]
    ybT = rv(24576, 8192, BF16, "p (k t) -> p k t", t=T); b_ybT = Buf("ybT")
    yaT = zaT; yaTb = zaTb
    xcs_st = [rv(32768 + i * 4096, 4096, BF16) for i in range(2)]; xcs_stb = [Buf(f"xcsst{i}") for i in range(2)]
    vtok = rv(40960, 4096, F32); b_vtok = Buf("vtok")
    tmpv = rv(45056, 4096, F32); b_tmpv = Buf("tmpv")
    vna = rv(49152, 2048, BF16); b_vna = Buf("vna")
    vnb = rv(51200, 2048, BF16); b_vnb = Buf("vnb")
    tmp2 = rv(53248, 4096, F32, "p (k t) -> p k t", t=128); b_tmp2 = Buf("tmp2")
    wst_stage = rv(0, 8192, F32, "p (h s) -> p h s", s=128); b_wst = Buf("wst_stage")
    zstage = [rv(i * 2048, 2048, F32) for i in range(4)]; zstageb = [Buf(f"zstage{i}") for i in range(4)]
    gstage = [rv(8192 + i * 2048, 2048, F32) for i in range(4)]; gstageb = [Buf(f"gstage{i}") for i in range(4)]
    t1 = [rv(16384 + i * 2048, 2048, F32) for i in range(2)]; t1b = [Buf(f"t1{i}") for i in range(2)]
    zst = [rv(i * 2112, 2112, F32) for i in range(4)]; zstb = [Buf(f"zst{i}") for i in range(4)]
    gst = [rv(8448 + i * 2048, 2048, F32) for i in range(4)]; gstb = [Buf(f"gst{i}") for i in range(4)]
    acc = [rv(16640 + i * 2048, 2048, F32) for i in range(2)]; accb = [Buf(f"acc{i}") for i in range(2)]
    ptok = rv(0, 4096, F32, "p (c d) -> p c d", d=256); b_ptok = Buf("ptok")
    pT = rv(4096, 2048, BF16, "p (k t) -> p k t", t=T); b_pT = Buf("pT")
    sg2 = [rv(6144 + i * 2048, 2048, F32) for i in range(2)]; sg2b = [Buf(f"sg2{i}") for i in range(2)]
    t2 = [rv(10240 + i * 2048, 2048, F32) for i in range(2)]; t2b = [Buf(f"t2{i}") for i in range(2)]

    region_state = {"bufs": []}

    def fence(new_bufs):
        old = region_state["bufs"]
        S.add("dve", lambda h: h.memset(dummy[:, 0:1], 0.0), writes=list(old) + list(new_bufs) + [b_dummy])
        region_state["bufs"] = list(new_bufs)

    ring = {"i": 0}

    def next_slot():
        i = ring["i"] % NSLOT
        ring["i"] += 1
        return i

    def wload(src, shape3, extra_reads=()):
        i = next_slot()
        a, b = shape3
        view = slots[i][:, 0:a * b].rearrange("p (a b) -> p a b", b=b)
        S.add("pool", lambda h: h.dma_start(out=view, in_=src), reads=list(extra_reads), writes=[slotb[i]],
              dma_sem=slot_sem[i])
        return view, slotb[i]

    def mm(out, lhsT, rhs, start, stop, reads, writes):
        S.add("pe", lambda h: h.matmul(out, lhsT=lhsT, rhs=rhs, start=start, stop=stop), reads=reads, writes=writes)

    def wview(w2d):
        return w2d.rearrange("(k p) n -> p k n", p=128)

    evac_rr = {"i": 0}

    def copy_evac(out, in_, reads, writes):
        evac_rr["i"] += 1
        if evac_rr["i"] % 2:
            S.add("act", lambda h: h.activation(out=out, in_=in_, func=AF.Copy), reads=reads, writes=writes)
        else:
            S.add("dve", lambda h: h.tensor_copy(out=out, in_=in_), reads=reads, writes=writes)

    S.add("sp", lambda h: h.dma_start(out=identf[:], in_=ident_in[:, :]), writes=[b_const], dma_sem=sem_misc)
    S.add("sp", lambda h: h.dma_start(out=d64[:], in_=d64_in[:, :]), writes=[b_const], dma_sem=sem_misc)
    S.add("sp", lambda h: h.dma_start(out=selh[:], in_=selh_in[:, :]), writes=[b_const], dma_sem=sem_misc)
    gsrc = [(g_ffn1, 0), (g_mix, 4), (g_ffn2, 8), (g_ple, 12)]
    for (gap, base) in gsrc:
        for l in range(4):
            S.add("sp", lambda h, gap=gap, base=base, l=l: h.dma_start(
                out=gvec[:, base + l, :], in_=gap[l].rearrange("(k p) -> p k", p=128), allow_slow_non_contiguous=True),
                writes=[b_const], dma_sem=sem_misc)
    S.add("sp", lambda h: h.dma_start(out=gvec[:, 16, :], in_=g_final.rearrange("(k p) -> p k", p=128),
                                      allow_slow_non_contiguous=True), writes=[b_const], dma_sem=sem_misc)
    for j in range(2):
        for t3 in range(3):
            S.add("sp", lambda h, j=j, t3=t3: h.dma_start(
                out=convw[:, j, t3, :], in_=w_conv[j, t3].rearrange("(k p) -> p k", p=128), allow_slow_non_contiguous=True),
                writes=[b_const], dma_sem=sem_misc)
        S.add("sp", lambda h, j=j: h.dma_start(
            out=convw[:, j, 3, :], in_=b_conv[j].rearrange("(k p) -> p k", p=128), allow_slow_non_contiguous=True),
            writes=[b_const], dma_sem=sem_misc)
    S.add("dve", lambda h: h.memset(ones_b[:], 1.0 / D), writes=[b_const])
    S.add("dve", lambda h: h.memset(zero_t[:], 0.0), writes=[b_const])
    S.add("dve", lambda h: h.memset(eps_t[:], EPS), writes=[b_const])
    for i in range(2):
        for col in (PADL - 1, PADL + 2048):
            S.add("sp", lambda h, i=i, col=col: h.dma_start(out=zs_p[i][:, :, col:col + 1], in_=zero_t[:],
                                                          allow_slow_non_contiguous=True),
                  reads=[b_const], writes=[b_zs[i]], dma_sem=sem_misc)

    def gv(idx):
        return gvec[:, idx, :]

    def rmsnorm_h(gidx, to_h=True, out_f32=None):
        PN = 6
        for kc in range(KC):
            s = kc % 4
            S.add("act", lambda h, kc=kc, s=s: h.activation(out=sq[s][:], in_=x_t[:, kc, :], func=AF.Square),
                  reads=[xb[kc]], writes=[sqb[s]])
            mm(bank(PN), ones_b[:], sq[s][:], kc == 0, kc == KC - 1, [sqb[s], b_const], [pb[PN]])
        S.add("act", lambda h: h.activation(out=rstd[:], in_=bank(PN), func=AF.Sqrt, bias=eps_t[:, 0:1], scale=1.0),
              reads=[pb[PN], b_const], writes=[b_rstd])
        S.add("dve", lambda h: h.reciprocal(out=rstd[:], in_=rstd[:]), reads=[b_rstd], writes=[b_rstd])
        g = gv(gidx)
        for kc in range(KC):
            if out_f32:
                S.add("dve", lambda h, kc=kc: h.scalar_tensor_tensor(out=x_t[:, kc, :], in0=x_t[:, kc, :], scalar=g[:, kc:kc + 1],
                                                                     in1=rstd[:], op0=ALU.mult, op1=ALU.mult),
                      reads=[xb[kc], b_rstd, b_const], writes=[xb[kc]])
            else:
                S.add("dve", lambda h, kc=kc: h.scalar_tensor_tensor(out=h_t[:, kc, :], in0=x_t[:, kc, :], scalar=g[:, kc:kc + 1],
                                                                     in1=rstd[:], op0=ALU.mult, op1=ALU.mult),
                      reads=[xb[kc], b_rstd, b_const], writes=[hb[kc]])

    def out_proj(w3, nk, rhs_fn, rhs_bufs, scale, k_off=0):
        for mg in range(4):
            bset = 4 if mg % 2 == 0 else 0
            k0 = 0
            while k0 < nk:
                n = min(8, nk - k0)
                wv, wb_ = wload(w3[:, k_off + k0:k_off + k0 + n, mg * 512:(mg + 1) * 512], (n, 512))
                for ki in range(n):
                    kc = k0 + ki
                    for mi in range(4):
                        mm(bank(bset + mi), wv[:, ki, mi * 128:(mi + 1) * 128], rhs_fn(kc), kc == 0, kc == nk - 1,
                           [wb_, rhs_bufs[kc]], [pb[bset + mi]])
                k0 += n
            for mi in range(4):
                m = mg * 4 + mi
                S.add("dve", lambda h, m=m, b=bset + mi: h.scalar_tensor_tensor(
                    out=x_t[:, m, :], in0=bank(b), scalar=float(scale), in1=x_t[:, m, :], op0=ALU.mult, op1=ALU.add),
                    reads=[pb[bset + mi], xb[m]], writes=[xb[m]])

    def ffn(l, which):
        w_in = (w_ffn1_in if which == 1 else w_ffn2_in)[l]
        w_out = (w_ffn1_out if which == 1 else w_ffn2_out)[l]
        w_in3 = wview(w_in)
        w_out3 = wview(w_out)
        rmsnorm_h((0 if which == 1 else 8) + l)
        fence(hidb + sgb)
        cnt = 0
        for u in range(FC // 2):
            gvw, gb_ = wload(w_in3[:, :, 256 * u:256 * u + 256], (KC, 256))
            uvw, ub_ = wload(w_in3[:, :, DFF + 256 * u:DFF + 256 * u + 256], (KC, 256))
            for jj in range(2):
                j = 2 * u + jj
                st = cnt % 2
                cnt += 1
                pg, pu = 2 * st, 2 * st + 1
                for kc in range(KC):
                    mm(bank(pg), gvw[:, kc, jj * 128:(jj + 1) * 128], h_t[:, kc, :], kc == 0, kc == KC - 1, [gb_, hb[kc]], [pb[pg]])
                for kc in range(KC):
                    mm(bank(pu), uvw[:, kc, jj * 128:(jj + 1) * 128], h_t[:, kc, :], kc == 0, kc == KC - 1, [ub_, hb[kc]], [pb[pu]])
                S.add("act", lambda h, st=st, pg=pg: h.activation(out=sgt[st], in_=bank(pg), func=AF.Silu),
                      reads=[pb[pg]], writes=[sgb[st]])
                S.add("dve", lambda h, st=st, pu=pu, j=j: h.tensor_tensor(out=hid[:, j, :], in0=sgt[st], in1=bank(pu), op=ALU.mult),
                      reads=[sgb[st], pb[pu]], writes=[hidb[j]])
        out_proj(w_out3, FC, lambda kc: hid[:, kc, :], hidb, 0.5)

    def load_input(tile):
        fence(tokb)
        for c in range(4):
            S.add("sp", lambda h, c=c: h.dma_start(out=tok[:, c, :], in_=xin[tile * T + c * 128: tile * T + (c + 1) * 128, :]),
                  writes=[tokb[c]], dma_sem=sem_tok[c])
        for kc in range(KC):
            b = kc % 8
            for c in range(4):
                S.add("pe", lambda h, kc=kc, c=c, b=b: h.transpose(out=PS[:, b, c * 128:(c + 1) * 128],
                                                                   in_=tok[:, c, kc * 128:(kc + 1) * 128], identity=identf[:]),
                      reads=[tokb[c], b_const], writes=[pb[b]])
            copy_evac(x_t[:, kc, :], bank(b), [pb[b]], [xb[kc]])

    def final_out(tile, raw=False):
        if not raw:
            rmsnorm_h(16, out_f32=True)
        fence(tokb)
        for c in range(4):
            for kg in range(4):
                b = (c * 4 + kg) % 8
                for q in range(4):
                    kc = kg * 4 + q
                    S.add("pe", lambda h, kc=kc, c=c, b=b, q=q: h.transpose(out=PS[:, b, q * 128:(q + 1) * 128],
                                                                          in_=x_t[:, kc, c * 128:(c + 1) * 128], identity=identf[:]),
                          reads=[xb[kc], b_const], writes=[pb[b]])
                copy_evac(tok[:, c, kg * 512:(kg + 1) * 512], bank(b), [pb[b]], [tokb[c]])
            S.add("sp", lambda h, c=c: h.dma_start(out=yout[tile * T + c * 128: tile * T + (c + 1) * 128, :], in_=tok[:, c, :]),
                  reads=[tokb[c]], dma_sem=sem_tok[4 + c])

    def load_x(tile):
        S.add("sp", lambda h: h.dma_start(out=x_t[:], in_=xs_dram[tile]), reads=[b_xs[tile]], writes=xb, dma_sem=sem_x)

    def store_x(tile):
        S.add("sp", lambda h: h.dma_start(out=xs_dram[tile], in_=x_t[:]), reads=xb, writes=[b_xs[tile]], dma_sem=sem_xst)

    def seq_of(tile):
        if tile < 4:
            return 0, tile
        if tile < 8:
            return 1, tile - 4
        return 2, tile - 8

    def even_consts(j):
        fence([b_wst])
        S.add("sp", lambda h: h.dma_start(out=wst_stage, in_=w_s[j].rearrange("h t s -> t h s")),
              writes=[b_wst], dma_sem=sem_misc)
        for hh in range(16):
            b = hh % 8
            S.add("pe", lambda h, hh=hh, b=b: h.transpose(out=PS[:, b, 0:128], in_=wst_stage[:, hh, :], identity=identf[:]),
                  reads=[b_wst, b_const], writes=[pb[b]])
            copy_evac(wsT[:, hh, :], PS[:, b, 0:128], [pb[b]], [b_evc])
        for hh in range(16):
            S.add("sp", lambda h, hh=hh: h.dma_start(out=bsT[(hh % 2) * 64:(hh % 2) * 64 + 64, hh // 2, :],
                                                     in_=b_s[j, hh:hh + 1, :].partition_broadcast(64)),
                  writes=[b_evc], dma_sem=sem_misc)
        gflat = g_v[j].rearrange("h c -> (h c)").unsqueeze(0)
        S.add("sp", lambda h: h.dma_start(out=gva[:], in_=gflat.partition_broadcast(128)), writes=[b_evc], dma_sem=sem_misc)
        S.add("sp", lambda h: h.dma_start(out=gvb[:], in_=gflat.partition_broadcast(128)), writes=[b_evc], dma_sem=sem_misc)
        S.add("dve", lambda h: h.memset(gva[:].rearrange("p (a b c) -> p a b c", b=2, c=64)[:, :, 1, :], 0.0),
              reads=[b_evc], writes=[b_evc])
        S.add("dve", lambda h: h.memset(gvb[:].rearrange("p (a b c) -> p a b c", b=2, c=64)[:, :, 0, :], 0.0),
              reads=[b_evc], writes=[b_evc])

    def mixab_in(l, tile):
        j = l // 2
        sid, ti = seq_of(tile)
        w3 = wview(w_in_ab[j])
        rmsnorm_h(4 + l)
        fence(uTb + zaTb + [b_ybT] + xcs_stb + [b_vtok, b_tmpv, b_vna, b_vnb, b_tmp2])
        for cb in range(8):
            wv, wb_ = wload(w3[:, :, cb * 256:(cb + 1) * 256], (KC, 256))
            for jj in range(2):
                c = cb * 2 + jj
                b = c % 4
                for kc in range(KC):
                    mm(bank(b), wv[:, kc, jj * 128:(jj + 1) * 128], h_t[:, kc, :], kc == 0, kc == KC - 1, [wb_, hb[kc]], [pb[b]])
                if c < 8:
                    copy_evac(zaT[:, c, :], bank(b), [pb[b]], [zaTb[c]])
                else:
                    S.add("act", lambda h, c=c, b=b: h.activation(out=uT[:, c - 8, :], in_=bank(b), func=AF.Gelu),
                          reads=[pb[b]], writes=[uTb[c - 8]])
        xdst = xcs_p[sid] if sid < 2 else xcs_sl.ap()
        for tcn in range(4):
            st = tcn % 2
            stv = xcs_st[st].rearrange("p (s c k) -> p s c k", s=2, c=8)
            for cp in range(4):
                b = 4 + (tcn * 4 + cp) % 4
                for q in range(2):
                    cc = cp * 2 + q
                    mm(PS[:, b, q * 256:(q + 1) * 256], zaT[:, cc, tcn * 128:(tcn + 1) * 128], d64[:], True, True,
                       [zaTb[cc], b_const], [pb[b]])
                src = PS[:, b, :].rearrange("p (q s k) -> p s q k", q=2, s=2)
                copy_evac(stv[:, :, cp * 2:cp * 2 + 2, :], src, [pb[b]], [xcs_stb[st]])
            r0 = ti * T + tcn * 128
            S.add("sp", lambda h, st=st, r0=r0: h.dma_start(out=xdst[r0:r0 + 128, :], in_=xcs_st[st]),
                  reads=[xcs_stb[st]], writes=[b_xcs[sid]], dma_sem=sem_scr[0])
        vw = {}
        for ch in range(2):
            for kh in range(2):
                vw[(ch, kh)] = wload(w3[:, kh * 8:(kh + 1) * 8, 2048 + ch * 512:2048 + (ch + 1) * 512], (8, 512))
        for tcn in range(4):
            for ch in range(2):
                b = ch
                for kc in range(KC):
                    wv, wb_ = vw[(ch, kc // 8)]
                    mm(bank(b), h_t[:, kc, tcn * 128:(tcn + 1) * 128], wv[:, kc % 8, :], kc == 0, kc == KC - 1, [wb_, hb[kc]], [pb[b]])
                S.add("act", lambda h, ch=ch, b=b: h.activation(out=vtok[:, ch * 512:(ch + 1) * 512], in_=bank(b), func=AF.Gelu),
                      reads=[pb[b]], writes=[b_vtok])
            S.add("dve", lambda h: h.tensor_tensor(out=tmpv, in0=vtok, in1=vtok, op=ALU.mult), reads=[b_vtok], writes=[b_tmpv])
            S.add("dve", lambda h: h.tensor_reduce(out=ss16[:], in_=tmpv.rearrange("p (a c) -> p a c", c=64), axis=AX.X, op=ALU.add),
                  reads=[b_tmpv], writes=[b_ss])
            S.add("act", lambda h: h.activation(out=rs16[:], in_=ss16[:], func=AF.Sqrt, bias=eps_t[:, 0:1], scale=1.0 / 64),
                  reads=[b_ss, b_const], writes=[b_rs])
            S.add("dve", lambda h: h.reciprocal(out=ss16[:], in_=rs16[:]), reads=[b_rs], writes=[b_ss])
            S.add("dve", lambda h: h.tensor_tensor(out=tmpv.rearrange("p (a c) -> p a c", c=64),
                                                   in0=vtok.rearrange("p (a c) -> p a c", c=64),
                                                   in1=ss16[:].unsqueeze(2).to_broadcast([128, 16, 64]), op=ALU.mult),
                  reads=[b_vtok, b_ss], writes=[b_tmpv])
            S.add("dve", lambda h: h.tensor_tensor(out=vna, in0=tmpv, in1=gva[:], op=ALU.mult), reads=[b_tmpv, b_evc], writes=[b_vna])
            S.add("dve", lambda h: h.tensor_tensor(out=vnb, in0=tmpv, in1=gvb[:], op=ALU.mult), reads=[b_tmpv, b_evc], writes=[b_vnb])
            for hp in range(8):
                b = 2 + hp // 4
                o = PS[:, b, (hp % 4) * 128:(hp % 4 + 1) * 128]
                mm(o, vna[:, hp * 128:(hp + 1) * 128], wsT[:, 2 * hp, :], True, False, [b_vna, b_evc], [pb[b]])
                mm(o, vnb[:, hp * 128:(hp + 1) * 128], wsT[:, 2 * hp + 1, :], False, True, [b_vnb, b_evc], [pb[b]])
            for bb in range(2):
                S.add("dve", lambda h, bb=bb: h.tensor_tensor(out=tmp2[:, bb * 4:(bb + 1) * 4, :],
                                                              in0=PS[:, 2 + bb, :].rearrange("p (q t) -> p q t", t=128),
                                                              in1=bsT[:, bb * 4:(bb + 1) * 4, :], op=ALU.add),
                      reads=[pb[2 + bb], b_evc], writes=[b_tmp2])
            S.add("dve", lambda h, tcn=tcn: h.tensor_tensor(out=ybT[:, :, tcn * 128:(tcn + 1) * 128], in0=tmp2,
                                                            in1=uT[:, :, tcn * 128:(tcn + 1) * 128], op=ALU.mult),
                  reads=[b_tmp2] + uTb, writes=[b_ybT])
        out_proj(wview(w_out_ab[j]), 8, lambda kc: ybT[:, kc, :], [b_ybT] * 8, 1.0, k_off=8)

    def mixab_out(l, tile):
        j = l // 2
        sid, ti = seq_of(tile)
        fence(yaTb)
        if sid < 2:
            n_t, xsrc, cm, sm, c0, xbuf = 16, xcs_p[sid], dftp_c, dftp_s, ti * T, b_xcs[sid]
        else:
            n_t, xsrc, cm, sm, c0, xbuf = 64, xcs_sg.ap(), dfts_c, dfts_s, ti * T, b_xcs_g
        for tch in range(n_t):
            i = next_slot()
            sl = slots[i]
            S.add("sp", lambda h, sl=sl, tch=tch: h.dma_start(out=sl[:, 0:2048], in_=xsrc[tch * 128:(tch + 1) * 128, :]),
                  reads=[xbuf], writes=[slotb[i]], dma_sem=slot_sem[i])
            S.add("sp", lambda h, sl=sl, tch=tch: h.dma_start(out=sl[:, 2048:2560], in_=cm[tch * 128:(tch + 1) * 128, c0:c0 + T]),
                  writes=[slotb[i]], dma_sem=slot_sem[i])
            S.add("sp", lambda h, sl=sl, tch=tch: h.dma_start(out=sl[:, 2560:3072], in_=sm[tch * 128:(tch + 1) * 128, c0:c0 + T]),
                  writes=[slotb[i]], dma_sem=slot_sem[i])
            for cc in range(8):
                mm(bank(cc), sl[:, cc * 128:(cc + 1) * 128], sl[:, 2048:2560], tch == 0, False, [slotb[i]], [pb[cc]])
                mm(bank(cc), sl[:, 1024 + cc * 128:1024 + (cc + 1) * 128], sl[:, 2560:3072], False, tch == n_t - 1, [slotb[i]], [pb[cc]])
        for cc in range(8):
            copy_evac(yaT[:, cc, :], bank(cc), [pb[cc]], [yaTb[cc]])
        out_proj(wview(w_out_ab[j]), 8, lambda kc: yaT[:, kc, :], yaTb, 1.0, k_off=0)

    def mixc_in(l, tile):
        j = l // 2
        sid, ti = seq_of(tile)
        w3 = wview(w_in_c[j])
        rmsnorm_h(4 + l)
        fence(zstageb + gstageb + t1b)
        zdst = zs_p[sid] if sid < 2 else zs_s
        cnt = 0
        for cb in range(8):
            gcw, gcb = wload(w3[:, :, D + cb * 256:D + (cb + 1) * 256], (KC, 256))
            xiw, xib = wload(w3[:, :, 2 * D + cb * 256:2 * D + (cb + 1) * 256], (KC, 256))
            gbw, gbb = wload(w3[:, :, cb * 256:(cb + 1) * 256], (KC, 256))
            for jj in range(2):
                c = cb * 2 + jj
                st = cnt % 2
                cnt += 1
                p0 = 3 * st
                for (wv, wb_, pbk) in ((gcw, gcb, p0), (xiw, xib, p0 + 1), (gbw, gbb, p0 + 2)):
                    for kc in range(KC):
                        mm(bank(pbk), wv[:, kc, jj * 128:(jj + 1) * 128], h_t[:, kc, :], kc == 0, kc == KC - 1, [wb_, hb[kc]], [pb[pbk]])
                zi = c % 4
                S.add("act", lambda h, st=st, p0=p0: h.activation(out=t1[st], in_=bank(p0 + 1), func=AF.Copy),
                      reads=[pb[p0 + 1]], writes=[t1b[st]])
                S.add("dve", lambda h, st=st, p0=p0, zi=zi: h.tensor_tensor(out=zstage[zi], in0=bank(p0), in1=t1[st], op=ALU.mult),
                      reads=[pb[p0], t1b[st]], writes=[zstageb[zi]])
                S.add("sp", lambda h, zi=zi, c=c: h.dma_start(out=zdst[:, c, PADL + ti * T:PADL + (ti + 1) * T], in_=zstage[zi]),
                      reads=[zstageb[zi]], writes=[b_zs[sid]], dma_sem=sem_stage[zi])
                if sid == 2:
                    ecol = 0 if ti == 0 else T - 1
                    S.add("act", lambda h, zi=zi, c=c, ecol=ecol: h.activation(out=edge_sb[:, ti, c:c + 1], in_=zstage[zi][:, ecol:ecol + 1],
                                                                                func=AF.Copy),
                          reads=[zstageb[zi]], writes=[b_edge_sb])
                S.add("act", lambda h, p0=p0, zi=zi: h.activation(out=gstage[zi], in_=bank(p0 + 2), func=AF.Copy),
                      reads=[pb[p0 + 2]], writes=[gstageb[zi]])
                S.add("sp", lambda h, zi=zi, c=c: h.dma_start(out=gbs[tile][:, c, :], in_=gstage[zi]),
                      reads=[gstageb[zi]], writes=[b_gbs[tile]], dma_sem=sem_stage[4 + zi])

    def halo_exchange():
        S.add("sp", lambda h: h.dma_start(out=edge_l.ap().rearrange("r (p k) -> p r k", k=KC), in_=edge_sb[:]),
              reads=[b_edge_sb], writes=[b_edge_l], dma_sem=sem_scr[1])
        S.add("pool", lambda h: h.collective_compute("AllGather", ALU.bypass, replica_groups=[list(range(NCORES))],
                                                     ins=[edge_l.ap().opt()], outs=[edge_g.ap().opt()]),
              reads=[b_edge_l], writes=[b_edge_g], dma_sem=sem_cc, dma_inc=1)
        S.add("sp", lambda h: h.dma_start(out=eg_sb[:], in_=edge_g.ap()[:, :]), reads=[b_edge_g], writes=[b_eg], dma_sem=sem_scr[2])
        for kc in range(KC):
            mm(PS[:, 7, kc * 2:kc * 2 + 2], eg_sb[:, :].rearrange("r (p k) -> r k p", k=KC)[:, kc, :], selh[:], True, True,
               [b_eg, b_const], [pb[7]])
        S.add("dve", lambda h: h.tensor_copy(out=halo[:], in_=PS[:, 7, 0:2 * KC].rearrange("p (k t) -> p k t", t=2)),
              reads=[pb[7]], writes=[b_halo])
        S.add("sp", lambda h: h.dma_start(out=zs_s[:, :, PADL - 1:PADL], in_=halo[:, :, 0:1], allow_slow_non_contiguous=True),
              reads=[b_halo], writes=[b_zs[2]], dma_sem=sem_scr[3])
        S.add("sp", lambda h: h.dma_start(out=zs_s[:, :, PADL + 1024:PADL + 1025], in_=halo[:, :, 1:2], allow_slow_non_contiguous=True),
              reads=[b_halo], writes=[b_zs[2]], dma_sem=sem_scr[3])

    def xcs_gather():
        S.add("pool", lambda h: h.collective_compute("AllGather", ALU.bypass, replica_groups=[list(range(NCORES))],
                                                     ins=[xcs_sl.ap().opt()], outs=[xcs_sg.ap().opt()]),
              reads=[b_xcs[2]], writes=[b_xcs_g], dma_sem=sem_cc, dma_inc=1)

    def mixc_out(l, tile):
        j = l // 2
        sid, ti = seq_of(tile)
        zsrc = zs_p[sid] if sid < 2 else zs_s
        fence(zstb + gstb + accb)
        for c in range(KC):
            zi = c % 4
            ai = c % 2
            zv = zst[zi][:, 0:514]
            S.add("sp", lambda h, zv=zv, c=c: h.dma_start(out=zv, in_=zsrc[:, c, PADL - 1 + ti * T:PADL - 1 + ti * T + 514]),
                  reads=[b_zs[sid]], writes=[zstb[zi]], dma_sem=sem_stage[zi])
            S.add("sp", lambda h, zi=zi, c=c: h.dma_start(out=gst[zi], in_=gbs[tile][:, c, :]),
                  reads=[b_gbs[tile]], writes=[gstb[zi]], dma_sem=sem_stage[4 + zi])
            w0 = convw[:, j, 0, c:c + 1]; w1 = convw[:, j, 1, c:c + 1]; w2 = convw[:, j, 2, c:c + 1]; bc = convw[:, j, 3, c:c + 1]
            S.add("dve", lambda h, zv=zv, ai=ai, w0=w0: h.tensor_scalar(out=acc[ai], in0=zv[:, 0:512], scalar1=w0, scalar2=None, op0=ALU.mult),
                  reads=[zstb[zi], b_const], writes=[accb[ai]])
            S.add("dve", lambda h, zv=zv, ai=ai, w1=w1: h.scalar_tensor_tensor(out=acc[ai], in0=zv[:, 1:513], scalar=w1, in1=acc[ai],
                                                                               op0=ALU.mult, op1=ALU.add),
                  reads=[zstb[zi], accb[ai]], writes=[accb[ai]])
            S.add("dve", lambda h, zv=zv, ai=ai, w2=w2: h.scalar_tensor_tensor(out=acc[ai], in0=zv[:, 2:514], scalar=w2, in1=acc[ai],
                                                                               op0=ALU.mult, op1=ALU.add),
                  reads=[zstb[zi], accb[ai]], writes=[accb[ai]])
            S.add("dve", lambda h, ai=ai, zi=zi, c=c, bc=bc: h.scalar_tensor_tensor(out=h_t[:, c, :], in0=acc[ai], scalar=bc, in1=gst[zi],
                                                                                    op0=ALU.add, op1=ALU.mult),
                  reads=[accb[ai], gstb[zi]], writes=[hb[c]])
        out_proj(wview(w_out_c[j]), KC, lambda kc: h_t[:, kc, :], hb, 1.0)

    def ple(l, tile):
        rmsnorm_h(12 + l)
        fence([b_ptok, b_pT] + sg2b + t2b)
        for c in range(4):
            S.add("sp", lambda h, c=c: h.dma_start(out=ptok[:, c, :], in_=pin[l, tile * T + c * 128: tile * T + (c + 1) * 128, :]),
                  writes=[b_ptok], dma_sem=sem_scr[4])
        for kk in range(2):
            b = 4 + kk
            for c in range(4):
                S.add("pe", lambda h, kk=kk, c=c, b=b: h.transpose(out=PS[:, b, c * 128:(c + 1) * 128],
                                                                   in_=ptok[:, c, kk * 128:(kk + 1) * 128], identity=identf[:]),
                      reads=[b_ptok, b_const], writes=[pb[b]])
            copy_evac(pT[:, kk, :], bank(b), [pb[b]], [b_pT])
        wg3 = wview(w_ple_gate[l])
        wp3 = wview(w_ple[l])
        pw, pwb = pw_t, b_pw
        S.add("pool", lambda h: h.dma_start(out=pw_t[:], in_=wp3[:, :, :]), writes=[b_pw], dma_sem=sem_pwl)
        cnt = 0
        for mb in range(8):
            gw, gwb = wload(wg3[:, :, mb * 256:(mb + 1) * 256], (KC, 256))
            for jj in range(2):
                m = mb * 2 + jj
                st = cnt % 2
                cnt += 1
                pg, pp = 2 * st, 2 * st + 1
                for kc in range(KC):
                    mm(bank(pg), gw[:, kc, jj * 128:(jj + 1) * 128], h_t[:, kc, :], kc == 0, kc == KC - 1, [gwb, hb[kc]], [pb[pg]])
                for kk in range(2):
                    mm(bank(pp), pw[:, kk, m * 128:(m + 1) * 128], pT[:, kk, :], kk == 0, kk == 1, [pwb, b_pT], [pb[pp]])
                S.add("act", lambda h, st=st, pg=pg: h.activation(out=sg2[st], in_=bank(pg), func=AF.Sigmoid),
                      reads=[pb[pg]], writes=[sg2b[st]])
                S.add("dve", lambda h, st=st, pp=pp: h.tensor_tensor(out=t2[st], in0=sg2[st], in1=bank(pp), op=ALU.mult),
                      reads=[sg2b[st], pb[pp]], writes=[t2b[st]])
                S.add("dve", lambda h, st=st, m=m: h.tensor_tensor(out=x_t[:, m, :], in0=t2[st], in1=x_t[:, m, :], op=ALU.add),
                      reads=[t2b[st], xb[m]], writes=[xb[m]])

    if segs is None:
        segs = [
            [("in",), ("ffn", 0, 1), ("abin", 0)],
            [("about", 0), ("ffn", 0, 2), ("ple", 0), ("ffn", 1, 1), ("cin", 1)],
            [("cout", 1), ("ffn", 1, 2), ("ple", 1), ("ffn", 2, 1), ("abin", 2)],
            [("about", 2), ("ffn", 2, 2), ("ple", 2), ("ffn", 3, 1), ("cin", 3)],
            [("cout", 3), ("ffn", 3, 2), ("ple", 3), ("final",)],
        ]
    for si, seg in enumerate(segs):
        kinds = [ph[0] for ph in seg]
        for ph in seg:
            if ph[0] == "abin":
                even_consts(ph[1] // 2)
        for tile in range(NTILE):
            if "in" not in kinds:
                load_x(tile)
            for ph in seg:
                k = ph[0]
                if k == "in":
                    load_input(tile)
                elif k == "ffn":
                    ffn(ph[1], ph[2])
                elif k == "abin":
                    mixab_in(ph[1], tile)
                elif k == "about":
                    mixab_out(ph[1], tile)
                elif k == "cin":
                    mixc_in(ph[1], tile)
                elif k == "cout":
                    mixc_out(ph[1], tile)
                elif k == "ple":
                    ple(ph[1], tile)
                elif k == "final":
                    final_out(tile, raw=(len(ph) > 1))
            if "final" not in kinds:
                store_x(tile)
        if "abin" in kinds:
            xcs_gather()
        if "cin" in kinds:
            halo_exchange()

    S.run(final_tokens=[("dma", sem_tok[4 + c], S.dma_cnt[sem_tok[4 + c]]) for c in range(4)])
    return nc


_DEBUG_SEGS = None


def _dft_tables():
    bf = ml_dtypes.bfloat16
    out = {}
    for n, key in ((2048, "p"), (8192, "s")):
        idx = np.arange(n, dtype=np.int64)
        prod = (idx[:, None] * idx[None, :]) % n
        ang = prod.astype(np.float64) * (2.0 * np.pi / n)
        sc = 1.0 / np.sqrt(n)
        out[key + "c"] = (np.cos(ang) * sc).astype(np.float32).astype(bf)
        out[key + "s"] = (-np.sin(ang) * sc).astype(np.float32).astype(bf)
    c = np.arange(64)
    ang = ((c[:, None] * c[None, :]) % 64) * (2.0 * np.pi / 64)
    cc = np.cos(ang) / 8.0
    sc_ = np.sin(ang) / 8.0
    d64 = np.zeros((128, 256), np.float32)
    d64[0:64, 0:64] = cc; d64[64:128, 64:128] = cc
    d64[0:64, 128:192] = sc_; d64[64:128, 192:256] = sc_
    out["d64"] = d64.astype(bf)
    return out


def kernel(x_prompt, x_sample, p_prompt, p_sample, g_ffn1, w_ffn1_in, w_ffn1_out, g_mix,
           w_in_ab, g_v, w_s, b_s, w_out_ab, w_in_c, w_conv, b_conv, w_out_c,
           g_ffn2, w_ffn2_in, w_ffn2_out, g_ple, w_ple_gate, w_ple, g_final):
    f = lambda a: np.ascontiguousarray(np.asarray(a, dtype=np.float32))
    x_prompt = f(x_prompt); x_sample = f(x_sample); p_prompt = f(p_prompt); p_sample = f(p_sample)
    import time as _time
    _t0 = _time.time()
    tabs = _dft_tables()
    print("[kernel] tables", _time.time() - _t0, flush=True)
    shared = {
        "g_ffn1": f(g_ffn1), "w_ffn1_in": f(w_ffn1_in), "w_ffn1_out": f(w_ffn1_out), "g_mix": f(g_mix),
        "w_in_ab": f(w_in_ab), "g_v": f(g_v), "w_s": f(w_s), "b_s": f(b_s), "w_out_ab": f(w_out_ab),
        "w_in_c": f(w_in_c), "w_conv": f(w_conv), "b_conv": f(b_conv), "w_out_c": f(w_out_c),
        "g_ffn2": f(g_ffn2), "w_ffn2_in": f(w_ffn2_in), "w_ffn2_out": f(w_ffn2_out), "g_ple": f(g_ple),
        "w_ple_gate": f(w_ple_gate), "w_ple": f(w_ple), "g_final": f(g_final),
        "dftp_c": tabs["pc"], "dftp_s": tabs["ps"], "d64": tabs["d64"],
        "identf": np.eye(128, dtype=np.float32),
    }
    in_maps = []
    for r in range(NCORES):
        xin = np.concatenate([x_prompt[2 * r], x_prompt[2 * r + 1], x_sample[0, 1024 * r:1024 * (r + 1)]], axis=0)
        pin = np.concatenate([p_prompt[:, 2 * r], p_prompt[:, 2 * r + 1], p_sample[:, 0, 1024 * r:1024 * (r + 1)]], axis=1)
        selh = np.zeros((16, 2), np.float32)
        if r > 0:
            selh[2 * (r - 1) + 1, 0] = 1.0
        if r < NCORES - 1:
            selh[2 * (r + 1), 1] = 1.0
        m = dict(shared)
        m["xin"] = np.ascontiguousarray(xin)
        m["pin"] = np.ascontiguousarray(pin)
        m["selh"] = selh
        m["dfts_c"] = np.ascontiguousarray(tabs["sc"][:, 1024 * r:1024 * (r + 1)])
        m["dfts_s"] = np.ascontiguousarray(tabs["ss"][:, 1024 * r:1024 * (r + 1)])
        in_maps.append(m)
    print("[kernel] in_maps", _time.time() - _t0, flush=True)
    nc = build_program(_DEBUG_SEGS)
    print("[kernel] built", _time.time() - _t0, flush=True)
    res = run_bass_kernel_spmd(nc, in_maps, core_ids=list(range(NCORES)))
    print("[kernel] ran", _time.time() - _t0, flush=True)
    y_prompt = np.empty((16, 2048, D), np.float32)
    y_sample = np.empty((1, 8192, D), np.float32)
    for r in range(NCORES):
        y = res.results[r]["yout"]
        y_prompt[2 * r] = y[0:2048]
        y_prompt[2 * r + 1] = y[2048:4096]
        y_sample[0, 1024 * r:1024 * (r + 1)] = y[4096:5120]
    return (y_prompt, y_sample)
```
